# Optimizing a Trainium2 kernel written in Bass

```python
import math
import jax, jax.numpy as jnp
from jax import lax
import numpy as np

D_MODEL = 2048
BATCH = 2
SEQ = 8192
DEPTH = 4
DEC_BATCH = 16
DEC_SEQ = 16
PAST_LEN = 1024

CHUNK = 64
N_MIXERS = 2
N_HEADS = 16
HEAD_DIM = D_MODEL // N_HEADS
SB_SCALE = 1.0 / math.sqrt(HEAD_DIM)
CONV_WIDTH = 31
D_FF = -(-8 * D_MODEL // (3 * 256)) * 256
Q_BLOCK = 128
EPS_RMS = 1e-6
EPS_LN = 1e-5
N_SB = (DEPTH + 1) // 2
N_CONV = DEPTH // 2

kernel_name = 'stickbreak_conformer_stream_step'


def rms_norm(x, g):
    xf = x.astype(jnp.float32)
    y = xf * lax.rsqrt(jnp.mean(xf * xf, axis=-1, keepdims=True) + EPS_RMS)
    return (y * g.astype(jnp.float32)).astype(x.dtype)


def layer_norm(x, g, b):
    xf = x.astype(jnp.float32)
    mu = jnp.mean(xf, axis=-1, keepdims=True)
    var = jnp.mean(jnp.square(xf - mu), axis=-1, keepdims=True)
    y = (xf - mu) * lax.rsqrt(var + EPS_LN)
    return (y * g.astype(jnp.float32) + b.astype(jnp.float32)).astype(x.dtype)


def swiglu_ffn(h, w_gate_up, w_down):
    gate, up = jnp.split(h @ w_gate_up, 2, axis=-1)
    return (jax.nn.silu(gate) * up) @ w_down


def sb_attend(q, k, v, q_pos):
    tk = k.shape[1]
    z = jnp.einsum('bqhd,bkhd->bhqk', q.astype(jnp.float32), k.astype(jnp.float32)) * SB_SCALE
    causal = jnp.arange(tk)[None, :] < q_pos[:, None]
    log_keep = jnp.where(causal, jax.nn.log_sigmoid(-z), 0.0)
    later = lax.cumsum(log_keep, axis=3, reverse=True) - log_keep
    w = jnp.where(causal, jnp.exp(jax.nn.log_sigmoid(z) + later), 0.0)
    return jnp.einsum('bhqk,bkhd->bqhd', w.astype(v.dtype), v)


def sb_qkv(h, w_qkv):
    b, t, _ = h.shape
    qkv = (h @ w_qkv).reshape(b, t, 3, N_HEADS, HEAD_DIM)
    return qkv[:, :, 0], qkv[:, :, 1], qkv[:, :, 2]


def sb_mixer_prompt(h, w_qkv, w_o):
    b, s, _ = h.shape
    q, k, v = sb_qkv(h, w_qkv)
    nb = s // Q_BLOCK
    qb = q.reshape(b, nb, Q_BLOCK, N_HEADS, HEAD_DIM).transpose(1, 0, 2, 3, 4)
    pos = jnp.arange(s).reshape(nb, Q_BLOCK)
    o = lax.map(lambda a: sb_attend(a[0], k, v, a[1]), (qb, pos))
    o = o.transpose(1, 0, 2, 3, 4).reshape(b, s, D_MODEL)
    return o @ w_o, k, v


def sb_mixer_sample(h, k_cache, v_cache, w_qkv, w_o):
    b, t, _ = h.shape
    q, k, v = sb_qkv(h, w_qkv)
    past = k_cache.shape[1]
    k_all = jnp.concatenate([k_cache.astype(k.dtype), k], axis=1)
    v_all = jnp.concatenate([v_cache.astype(v.dtype), v], axis=1)
    o = sb_attend(q, k_all, v_all, past + jnp.arange(t)).reshape(b, t, D_MODEL)
    return o @ w_o, k, v


def conv_module(h, buf, w_pw1, b_pw1, w_dw, b_dw, ln_g, ln_b, w_pw2, b_pw2):
    a, g = jnp.split(h @ w_pw1 + b_pw1, 2, axis=-1)
    u = a * jax.nn.sigmoid(g)
    up = jnp.concatenate([buf.astype(u.dtype), u], axis=1)
    y = lax.conv_general_dilated(up, w_dw[:, None, :].astype(up.dtype), (1,), 'VALID',
                                 dimension_numbers=('NWC', 'WIO', 'NWC'),
                                 feature_group_count=D_MODEL) + b_dw
    y = jax.nn.silu(layer_norm(y, ln_g, ln_b))
    return y @ w_pw2 + b_pw2, up[:, -(CONV_WIDTH - 1):]


def _trunk(x, cache_k, cache_v, state_conv, norm_mix_g, norm_ffn_g, w_qkv, w_o,
           w_pw1, b_pw1, w_dw, b_dw, ln_g, ln_b, w_pw2, b_pw2,
           w_gate_up, w_down, final_norm_g):
    new_k, new_v, new_conv = [], [], []
    for i in range(DEPTH):
        h = rms_norm(x, norm_mix_g[i])
        j = i // N_MIXERS
        if i % N_MIXERS == 0:
            if cache_k is None:
                out, k, v = sb_mixer_prompt(h, w_qkv[j], w_o[j])
            else:
                out, k, v = sb_mixer_sample(h, cache_k[j], cache_v[j], w_qkv[j], w_o[j])
            new_k.append(k)
            new_v.append(v)
        else:
            if state_conv is None:
                buf = jnp.zeros((x.shape[0], CONV_WIDTH - 1, D_MODEL), x.dtype)
            else:
                buf = state_conv[j]
            out, nbuf = conv_module(h, buf, w_pw1[j], b_pw1[j], w_dw[j], b_dw[j],
                                    ln_g[j], ln_b[j], w_pw2[j], b_pw2[j])
            new_conv.append(nbuf)
        x = x + out
        x = x + swiglu_ffn(rms_norm(x, norm_ffn_g[i]), w_gate_up[i], w_down[i])
    return rms_norm(x, final_norm_g), jnp.stack(new_k), jnp.stack(new_v), jnp.stack(new_conv)


def setup_inputs(seed: int = 0) -> dict:
    key = jax.random.key(seed)
    ks = jax.random.split(key, 20)
    D, F, W = D_MODEL, D_FF, CONV_WIDTH

    def nrm(k, shape, scale):
        return jax.random.normal(k, shape, jnp.float32) * scale

    return {
        'x_prompt': nrm(ks[0], (BATCH, SEQ, D), 1.0),
        'x_sample': nrm(ks[1], (DEC_BATCH, DEC_SEQ, D), 1.0),
        'cache_k': nrm(ks[2], (N_SB, DEC_BATCH, PAST_LEN, N_HEADS, HEAD_DIM), 1.0),
        'cache_v': nrm(ks[3], (N_SB, DEC_BATCH, PAST_LEN, N_HEADS, HEAD_DIM), 1.0),
        'state_conv': nrm(ks[4], (N_CONV, DEC_BATCH, W - 1, D), 0.5),
        'norm_mix_g': 1.0 + nrm(ks[5], (DEPTH, D), 0.02),
        'norm_ffn_g': 1.0 + nrm(ks[6], (DEPTH, D), 0.02),
        'w_qkv': nrm(ks[7], (N_SB, D, 3 * D), D ** -0.5),
        'w_o': nrm(ks[8], (N_SB, D, D), D ** -0.5),
        'w_pw1': nrm(ks[9], (N_CONV, D, 2 * D), D ** -0.5),
        'b_pw1': nrm(ks[10], (N_CONV, 2 * D), 0.02),
        'w_dw': nrm(ks[11], (N_CONV, W, D), W ** -0.5),
        'b_dw': nrm(ks[12], (N_CONV, D), 0.02),
        'ln_g': 1.0 + nrm(ks[13], (N_CONV, D), 0.02),
        'ln_b': nrm(ks[14], (N_CONV, D), 0.02),
        'w_pw2': nrm(ks[15], (N_CONV, D, D), D ** -0.5),
        'b_pw2': nrm(ks[16], (N_CONV, D), 0.02),
        'w_gate_up': nrm(ks[17], (DEPTH, D, 2 * F), D ** -0.5),
        'w_down': nrm(ks[18], (DEPTH, F, D), F ** -0.5),
        'final_norm_g': 1.0 + nrm(ks[19], (D,), 0.02),
    }


def reference(x_prompt, x_sample, cache_k, cache_v, state_conv, norm_mix_g, norm_ffn_g,
              w_qkv, w_o, w_pw1, b_pw1, w_dw, b_dw, ln_g, ln_b, w_pw2, b_pw2,
              w_gate_up, w_down, final_norm_g):
    y_prompt, k_p, v_p, conv_p = _trunk(x_prompt, None, None, None, norm_mix_g, norm_ffn_g,
                                        w_qkv, w_o, w_pw1, b_pw1, w_dw, b_dw, ln_g, ln_b,
                                        w_pw2, b_pw2, w_gate_up, w_down, final_norm_g)
    y_sample, k_s, v_s, conv_s = _trunk(x_sample, cache_k, cache_v, state_conv, norm_mix_g,
                                        norm_ffn_g, w_qkv, w_o, w_pw1, b_pw1, w_dw, b_dw,
                                        ln_g, ln_b, w_pw2, b_pw2, w_gate_up, w_down,
                                        final_norm_g)
    return (y_prompt, y_sample, k_p, v_p, conv_p, k_s, v_s, conv_s)
```

```python
import math
import os
from contextlib import ExitStack

import numpy as np
import concourse.bass as bass
import concourse.mybir as mybir
from concourse.bass_utils import run_bass_kernel_spmd

F32 = mybir.dt.float32
BF16 = mybir.dt.bfloat16
AF = mybir.ActivationFunctionType
ALU = mybir.AluOpType

D = 2048
KC = 16
DFF = 5632
FC = 44
NH = 16
PT = 2048
ST = 32
NTOK = PT + ST
GT = 4 * NTOK
SEQ = 8192
CW = 31
SB_SCALE = 1.0 / math.sqrt(128.0)
EPS_RMS = 1e-6
EPS_LN = 1e-5
GROUPS = [[0, 1, 2, 3], [4, 5, 6, 7]]
TILES = [(0, 512), (512, 512), (1024, 512), (1536, 512), (2048, 32)]
NV = 83
CM = 128 + 128 + 2048 + 128
STOP = int(os.environ.get("MK_STOP", "0")) or None
LITE = STOP is not None and STOP <= 7
DBG_SUB = os.environ.get("MK_SUB", "qvsad")
DBG_NT = int(os.environ.get("MK_NT", "99"))


def V_MIX(i): return i
def V_FFN(i): return 4 + i
V_FIN = 8
def V_BPW1(j, h): return 9 + 2 * j + h
def V_BDW(j): return 13 + j
def V_LNG(j): return 15 + j
def V_LNB(j): return 17 + j
def V_BPW2(j): return 19 + j
def V_WDW(j, w): return 21 + 31 * j + w


class Op:
    __slots__ = ("eng", "fn", "deps", "sem", "val", "needed", "key", "inc", "grp")


class Sched:
    def __init__(self):
        self.ops = []
        self.lastw = {}
        self.rd_c = {}
        self.rd_d = {}
        self.last_eng = {}
        self.last_key = {}
        self.stopped = False
        self.nbar = 0

    def add(self, eng, fn, reads=(), writes=(), key=None, inc=16, grp=None):
        if self.stopped:
            return None
        op = Op()
        op.eng, op.fn, op.key, op.inc = eng, fn, key, inc
        op.grp = grp
        op.needed = False
        op.sem = None
        op.val = 0
        deps = []
        for t in reads:
            w = self.lastw.get(t)
            if w is not None:
                deps.append(w)
            if isinstance(t, tuple) and t[0] == "ps":
                for e2, o2 in self.rd_c.get(t, {}).items():
                    if e2 != eng:
                        deps.append(o2)
        for t in writes:
            w = self.lastw.get(t)
            if w is not None:
                deps.append(w)
            deps.extend(self.rd_c.get(t, {}).values())
            deps.extend(self.rd_d.get(t, ()))
        for t in reads:
            if key is None:
                self.rd_c.setdefault(t, {})[eng] = op
            else:
                self.rd_d.setdefault(t, []).append(op)
        for t in writes:
            self.lastw[t] = op
            self.rd_c[t] = {}
            self.rd_d[t] = []
        dd = []
        seen = set()
        for d in deps:
            if d is op or id(d) in seen:
                continue
            seen.add(id(d))
            if d.eng == "pe" and eng == "pe":
                continue
            dd.append(d)
        op.deps = dd
        for d in dd:
            d.needed = True
        self.ops.append(op)
        if key is None:
            self.last_eng[eng] = op
        else:
            self.last_key[key] = op
        return op

    def barrier(self, engines=("pe", "act", "dve", "pool", "sp"), force=False, label=""):
        if self.stopped and not force:
            return
        self.nbar += 1
        if STOP is not None:
            print(f"[barrier {self.nbar}] {label} ops={len(self.ops)}")
            if self.nbar >= STOP:
                self.stopped = True
        targets = list(self.last_eng.values()) + list(self.last_key.values())
        for e in engines:
            op = Op()
            op.eng, op.fn, op.key, op.inc = e, None, None, 0
            op.grp = None
            op.needed = False
            op.sem = None
            op.val = 0
            op.deps = [t for t in targets]
            for t in targets:
                t.needed = True
            self.ops.append(op)
        self.lastw = {}
        self.rd_c = {}
        self.rd_d = {}

    def assign(self):
        self.nsem = 0
        prog = {}
        cnt = {}
        ksem = {}
        kcnt = {}
        groups = {}
        for op in self.ops:
            if op.fn is None:
                continue
            if op.key is None:
                if not op.needed:
                    continue
                if op.eng not in prog or cnt[op.eng] >= 20000:
                    prog[op.eng] = self.nsem
                    self.nsem += 1
                    cnt[op.eng] = 0
                cnt[op.eng] += 1
                op.sem, op.val = prog[op.eng], cnt[op.eng]
            else:
                if op.key not in ksem:
                    ksem[op.key] = self.nsem
                    self.nsem += 1
                    kcnt[op.key] = 0
                kcnt[op.key] += op.inc
                op.sem, op.val = ksem[op.key], kcnt[op.key]
                if op.grp is not None:
                    groups.setdefault((op.key, op.grp), []).append(op)
        for ops in groups.values():
            v = max(o.val for o in ops)
            for o in ops:
                o.val = v
        return self.nsem

    def run_engine(self, eng, eh, sems):
        waited = {}
        for op in self.ops:
            if op.eng != eng:
                continue
            need = {}
            for d in op.deps:
                if d.sem is None:
                    continue
                if d.val > need.get(d.sem, 0):
                    need[d.sem] = d.val
            for sm, v in need.items():
                if waited.get(sm, 0) >= v:
                    continue
                eh.wait_ge(sems[sm], v)
                waited[sm] = v
            if op.fn is None:
                continue
            ins = op.fn(eh)
            if op.key is not None:
                if op.inc == 1:
                    ins.then_inc(sems[op.sem])
                else:
                    ins.then_inc(sems[op.sem], op.inc)
            elif op.needed:
                ins.then_inc(sems[op.sem], 1)


class Arena:
    def __init__(self, nc, nbytes):
        self.t = nc.alloc_sbuf_tensor("arena", [128, nbytes // 2], BF16)
        self.ap = self.t.ap()
        self.cap = nbytes
        self.off = 0
        self.base = 0

    def reset(self):
        self.off = self.base

    def alloc(self, free, dtype):
        n = 1
        for s in free:
            n *= s
        size = n * (4 if dtype is F32 else 2)
        off = (self.off + 63) // 64 * 64
        self.off = off + size
        assert self.off <= self.cap, (self.off, self.cap)
        v = self.ap[:, off // 2: off // 2 + size // 2]
        if dtype is F32:
            v = v.bitcast(F32)
        if len(free) == 2:
            v = v.rearrange("p (a b) -> p a b", a=free[0])
        elif len(free) == 3:
            v = v.rearrange("p (a b c) -> p a b c", a=free[0], b=free[1])
        return v


def build_program():
    nc = bass.Bass("TRN2", target_bir_lowering=False)
    S = Sched()

    def din(name, shape, dt=F32):
        return nc.dram_tensor(name, shape, dt, kind="ExternalInput").ap()

    def dout(name, shape, dt=F32):
        return nc.dram_tensor(name, shape, dt, kind="ExternalOutput").ap()

    def dint(name, shape, dt):
        if STOP is not None and name in ("xres", "qT_d", "kT_d", "v_d", "oT_d", "aT_d", "u_d", "yc_d"):
            return nc.dram_tensor(name, shape, dt, kind="ExternalOutput").ap()
        return nc.dram_tensor(name, shape, dt).ap()

    xT_in = din("xT", [D, NTOK])
    vecs_in = din("vecs", [128, NV * 16])
    cmat_in = din("cmat", [128, CM])
    sel_in = din("sel", [128, 4])
    wqkv_in = din("wqkv", [2, 128, 16 * 1536])
    wo_in = din("wo", [2, 128, 4 * D])
    ckT_in = din("ckT", [2, 4, 128, 8 * 1024])
    cv_in = din("cv", [2, 4, 128, 8 * 8 * 128])
    if LITE:
        wgu_in = wd_in = wpw1_in = wpw2_in = None
    else:
        wgu_in = din("wgu", [4, 22, 128, 16 * 512])
        wd_in = din("wd", [4, 8, 128, FC * 256])
        wpw1_in = din("wpw1", [2, 16, 128, 16 * 256])
        wpw2_in = din("wpw2", [2, 8, 128, 16 * 256])
    stT_in = din("stT", [2, D, 60])

    y_out = dout("y_out", [D, NTOK])
    k_out = dout("k_out", [2, 512, GT])
    v_out = dout("v_out", [2, GT, 512])
    convp_out = dout("convp_out", [2, D, 30])
    convs_out = dout("convs_out", [2, D, 60])

    xres = dint("xres", [D, NTOK], F32)
    hsrc = [dint(f"hsrc{c}", [128, NTOK], BF16) for c in range(KC)]
    hall = [dint(f"hall{c}", [4 * 128, NTOK], BF16) for c in range(KC)]
    qT_d = dint("qT_d", [512, GT], BF16)
    kT_d = dint("kT_d", [512, GT], BF16)
    v_d = dint("v_d", [GT, 512], BF16)
    oT_d = dint("oT_d", [512, GT], BF16)
    wop_d = dint("wop_d", [4 * D, NTOK], F32)
    wosum_d = dint("wosum_d", [D, NTOK], F32)
    aT_d = dint("aT_d", [DFF, NTOK], BF16)
    u_d = dint("u_d", [D, NTOK], F32)
    utail_d = dint("utail_d", [D, 30], F32)
    uhalo_d = dint("uhalo_d", [4 * D, 30], F32)
    yc_d = dint("yc_d", [D, NTOK], F32)

    A = Arena(nc, 206 * 1024)
    psum = nc.alloc_psum_tensor("ps", [128, 8, 512], F32).ap()

    vecs = A.alloc((NV, 16), F32)
    cm = A.alloc((CM,), BF16)
    sel = A.alloc((4,), F32)
    ones = cm[:, 0:128]
    tri = cm[:, 128:256]
    masks = cm[:, 256:256 + 2048].rearrange("p (m q) -> p m q", m=4)
    mask16 = cm[:, 2304:2432]
    zt = A.alloc((128,), BF16)
    A.base = A.off
    S.add("pool", lambda e: e.memset(zt, 0.0), writes=["zt"])

    S.add("sp", lambda e: e.dma_start(out=vecs, in_=vecs_in.rearrange("p (v c) -> p v c", v=NV)),
          writes=["vecs"], key="c0")
    S.add("sp", lambda e: e.dma_start(out=sel, in_=sel_in), writes=["sel"], key="c1")
    S.add("pool", lambda e: e.dma_start(out=cm, in_=cmat_in, max_dma_last_dim=4096), writes=["cm"], key="c2")
    S.barrier()

    def fm(ap2d):
        return ap2d.rearrange("(c p) t -> p c t", p=128)

    def phase_norm(hT, src, g_vec, add=None, write_back=None, mode="rms", ln=None, final_out=None):
        xt = [A.alloc((16, 512), F32) for _ in range(2)]
        dt_ = A.alloc((16, 512), F32) if add is not None else None
        sq = A.alloc((16, 512), BF16)
        y16 = A.alloc((16, 512), BF16) if mode == "ln" else None
        rs = [A.alloc((512,), F32) for _ in range(2)]
        mu = [A.alloc((512,), F32) for _ in range(2)] if mode == "ln" else None
        tmp = [A.alloc((512,), F32) for _ in range(2)] if mode == "ln" else None
        fo = [A.alloc((16, 512), F32) for _ in range(1)] if final_out is not None else None
        srcv = fm(src)
        for ti, (t0, n) in enumerate(TILES):
            sl = ti % 2
            x = xt[sl]
            S.add("sp", lambda e, x=x, t0=t0, n=n: e.dma_start(out=x[:, :, :n], in_=srcv[:, :, t0:t0 + n]),
                  writes=[("xt", sl)], key=f"xt{sl}")
            if add is not None:
                av = fm(add)
                S.add("sp", lambda e, t0=t0, n=n: e.dma_start(out=dt_[:, :, :n], in_=av[:, :, t0:t0 + n]),
                      writes=["dt"], key="dt")
                S.add("dve", lambda e, x=x, n=n: e.tensor_tensor(out=x[:, :, :n], in0=x[:, :, :n],
                                                                   in1=dt_[:, :, :n], op=ALU.add),
                      reads=["dt", ("xt", sl)], writes=[("xt", sl)])
                S.add("sp", lambda e, x=x, t0=t0, n=n: e.dma_start(out=fm(write_back)[:, :, t0:t0 + n],
                                                                     in_=x[:, :, :n]),
                      reads=[("xt", sl)], key=f"xt{sl}")
            S.add("act", lambda e, x=x, n=n: e.activation(out=sq[:, :, :n], in_=x[:, :, :n], func=AF.Square),
                  reads=[("xt", sl)], writes=["sq"])

            def ssum(e, n=n, bank=0, srcb=sq):
                ins = None
                for c in range(KC):
                    ins = e.matmul(psum[:, bank, :n], ones, srcb[:, c, :n], start=(c == 0), stop=(c == KC - 1))
                return ins
            S.add("pe", lambda e, f=ssum: f(e, bank=0, srcb=sq), reads=["sq", "cm"], writes=[("ps", 0)])
            r = rs[sl]
            if mode == "rms":
                S.add("act", lambda e, r=r, n=n: e.activation(out=r[:, :n], in_=psum[:, 0, :n], func=AF.Ln,
                                                               bias=EPS_RMS, scale=1.0 / D),
                      reads=[("ps", 0)], writes=[("rs", sl)])
                S.add("act", lambda e, r=r, n=n: e.activation(out=r[:, :n], in_=r[:, :n], func=AF.Exp, scale=-0.5),
                      reads=[("rs", sl)], writes=[("rs", sl)])
            else:
                m = mu[sl]
                tp = tmp[sl]
                S.add("act", lambda e, x=x, n=n: e.activation(out=y16[:, :, :n], in_=x[:, :, :n], func=AF.Copy),
                      reads=[("xt", sl)], writes=["y16"])
                S.add("pe", lambda e, f=ssum: f(e, bank=1, srcb=y16), reads=["y16", "cm"], writes=[("ps", 1)])
                S.add("dve", lambda e, m=m, n=n: e.tensor_scalar(out=m[:, :n], in0=psum[:, 1, :n],
                                                                  scalar1=1.0 / D, scalar2=0.0,
                                                                  op0=ALU.mult, op1=ALU.add),
                      reads=[("ps", 1)], writes=[("mu", sl)])
                S.add("dve", lambda e, m=m, tp=tp, n=n: e.tensor_tensor(out=tp[:, :n], in0=m[:, :n],
                                                                          in1=m[:, :n], op=ALU.mult),
                      reads=[("mu", sl)], writes=[("tmp", sl)])
                S.add("dve", lambda e, r=r, tp=tp, n=n: e.scalar_tensor_tensor(
                    out=r[:, :n], in0=psum[:, 0, :n], scalar=1.0 / D, in1=tp[:, :n],
                    op0=ALU.mult, op1=ALU.subtract),
                    reads=[("ps", 0), ("tmp", sl)], writes=[("rs", sl)])
                S.add("act", lambda e, r=r, n=n: e.activation(out=r[:, :n], in_=r[:, :n], func=AF.Ln,
                                                               bias=EPS_LN, scale=1.0),
                      reads=[("rs", sl)], writes=[("rs", sl)])
                S.add("act", lambda e, r=r, n=n: e.activation(out=r[:, :n], in_=r[:, :n], func=AF.Exp, scale=-0.5),
                      reads=[("rs", sl)], writes=[("rs", sl)])
            for c in range(KC):
                eng = "dve" if (c % 2 == 0 or mode == "rms") else "pool"
                if mode == "rms":
                    if final_out is None:
                        S.add(eng, lambda e, x=x, r=r, c=c, t0=t0, n=n: e.scalar_tensor_tensor(
                            out=hT[:, c, t0:t0 + n], in0=x[:, c, :n], scalar=vecs[:, g_vec, c:c + 1],
                            in1=r[:, :n], op0=ALU.mult, op1=ALU.mult),
                            reads=[("xt", sl), ("rs", sl), "vecs"], writes=[("hT", ti, c)])
                    else:
                        S.add(eng, lambda e, x=x, r=r, c=c, n=n: e.scalar_tensor_tensor(
                            out=fo[0][:, c, :n], in0=x[:, c, :n], scalar=vecs[:, g_vec, c:c + 1],
                            in1=r[:, :n], op0=ALU.mult, op1=ALU.mult),
                            reads=[("xt", sl), ("rs", sl), "vecs"], writes=[("fo", c)])
                else:
                    lng, lnb = ln
                    S.add(eng, lambda e, x=x, m=m, c=c, n=n: e.tensor_tensor(
                        out=x[:, c, :n], in0=x[:, c, :n], in1=m[:, :n], op=ALU.subtract),
                        reads=[("xt", sl), ("mu", sl)], writes=[("xtc", sl, c)])
                    S.add(eng, lambda e, x=x, r=r, c=c, n=n: e.tensor_tensor(
                        out=x[:, c, :n], in0=x[:, c, :n], in1=r[:, :n], op=ALU.mult),
                        reads=[("xtc", sl, c), ("rs", sl)], writes=[("xtc", sl, c)])
                    S.add("act", lambda e, x=x, c=c, t0=t0, n=n: e.activation(
                        out=hT[:, c, t0:t0 + n], in_=x[:, c, :n], func=AF.Silu,
                        bias=vecs[:, lnb, c:c + 1], scale=vecs[:, lng, c:c + 1]),
                        reads=[("xtc", sl, c), "vecs"], writes=[("hT", ti, c)])
            if mode == "ln":
                S.add("dve", lambda e, tp=tp: e.tensor_copy(out=tp[:, 0:1], in_=tp[:, 0:1]),
                      reads=[("tmp", sl)], writes=[("xt", sl), ("tmp", sl)] + [("xtc", sl, c) for c in range(KC)])
            if final_out is not None:
                S.add("sp", lambda e, t0=t0, n=n: e.dma_start(out=fm(final_out)[:, :, t0:t0 + n],
                                                               in_=fo[0][:, :, :n]),
                      reads=[("fo", c) for c in range(KC)], key="fo")

    def load_wblock(buf, src_ap, kc, nw, tok, key):
        S.add("pool", lambda e: e.dma_start(out=buf, in_=src_ap.rearrange("p (c n) -> p c n", c=kc),
                                            max_dma_last_dim=8192),
              writes=[tok], key=key)

    def mm_group(e, bank, n, wb, cols, rhs_fn, kc):
        ins = None
        for c in range(kc):
            ins = e.matmul(psum[:, bank, :n], wb[:, c, cols:cols + 128], rhs_fn(c), start=(c == 0), stop=(c == kc - 1))
        return ins

    def phase_ffn(layer, add):
        A.reset()
        hT = A.alloc((16, NTOK), BF16)
        mark = A.off
        phase_norm(hT, xres, V_FFN(layer), add=add, write_back=xres if add is not None else None)
        S.barrier()
        A.off = mark
        NWB = 3
        wb = [A.alloc((16, 512), BF16) for _ in range(NWB)]
        ast = [A.alloc((2, NTOK), BF16) for _ in range(2)]
        sg = [A.alloc((512,), F32) for _ in range(2)]
        pcount = 0

        def gu_load(blk):
            load_wblock(wb[blk % NWB], wgu_in[layer, blk], 16, 512, ("wb", blk % NWB), f"wb{blk % NWB}")
        gu_load(0)
        gu_load(1)
        for blk in range(22):
            ws = blk % NWB
            a_s = blk % 2
            if blk + 2 < 22:
                gu_load(blk + 2)
            for ti, (t0, n) in enumerate(TILES):
                for q in range(2):
                    bg = 2 * (pcount % 2)
                    bu = bg + 1
                    pcount += 1
                    s = sg[pcount % 2]
                    rhs = (lambda c, t0=t0, n=n: hT[:, c, t0:t0 + n])
                    S.add("pe", lambda e, bg=bg, n=n, w=wb[ws], q=q, rhs=rhs: mm_group(e, bg, n, w, q * 128, rhs, 16),
                          reads=[("wb", ws)] + [("hT", ti, c) for c in range(KC)], writes=[("ps", bg)])
                    S.add("pe", lambda e, bu=bu, n=n, w=wb[ws], q=q, rhs=rhs: mm_group(e, bu, n, w, 256 + q * 128, rhs, 16),
                          reads=[("wb", ws)] + [("hT", ti, c) for c in range(KC)], writes=[("ps", bu)])
                    S.add("act", lambda e, s=s, bg=bg, n=n: e.activation(out=s[:, :n], in_=psum[:, bg, :n], func=AF.Silu),
                          reads=[("ps", bg)], writes=[("sg", pcount % 2)])
                    S.add("dve", lambda e, s=s, bu=bu, n=n, a_s=a_s, q=q, t0=t0: e.tensor_tensor(
                        out=ast[a_s][:, q, t0:t0 + n], in0=s[:, :n], in1=psum[:, bu, :n], op=ALU.mult),
                        reads=[("ps", bu), ("sg", pcount % 2)], writes=[("ast", a_s, q, ti)])
            S.add("sp", lambda e, a_s=a_s, blk=blk: e.dma_start(
                out=aT_d[blk * 256:(blk + 1) * 256, :].rearrange("(q p) t -> p q t", p=128), in_=ast[a_s]),
                reads=[("ast", a_s, q, ti) for q in range(2) for ti in range(5)], writes=[("aT_d", blk)],
                key=f"ast{a_s}")
        S.barrier()
        A.off = A.base
        halves = [(0, 1024, [0, 1]), (1024, 1056, [2, 3, 4])]
        ah = A.alloc((FC, 1056), BF16)
        wdb = [A.alloc((FC, 256), BF16) for _ in range(2)]
        xr = [A.alloc((512,), F32) for _ in range(3)]
        xcount = 0
        for hi, (c0, ncol, tl) in enumerate(halves):
            S.add("sp", lambda e, c0=c0, ncol=ncol: e.dma_start(out=ah[:, :, :ncol], in_=fm(aT_d)[:, :, c0:c0 + ncol]),
                  writes=["ah"], key="ah")
            for db in range(8):
                ws = (hi * 8 + db) % 2
                if hi == 0 and db == 0:
                    load_wblock(wdb[0], wd_in[layer, 0], FC, 256, ("wdb", 0), "wdb0")
                nxt = hi * 8 + db + 1
                if nxt < 16:
                    load_wblock(wdb[nxt % 2], wd_in[layer, nxt % 8], FC, 256, ("wdb", nxt % 2), f"wdb{nxt % 2}")
                for ti in tl:
                    t0, n = TILES[ti]
                    for q in range(2):
                        dch = db * 2 + q
                        bank = xcount % 4
                        xs = xcount % 3
                        xcount += 1
                        S.add("sp", lambda e, xs=xs, dch=dch, t0=t0, n=n: e.dma_start(
                            out=xr[xs][:, :n], in_=xres[dch * 128:(dch + 1) * 128, t0:t0 + n]),
                            reads=[("xres", dch, ti)], writes=[("xr", xs)], key=f"xr{xs}")
                        rhs = (lambda c, t0=t0, n=n, c0=c0: ah[:, c, t0 - c0:t0 - c0 + n])
                        S.add("pe", lambda e, bank=bank, n=n, w=wdb[ws], q=q, rhs=rhs: mm_group(e, bank, n, w, q * 128, rhs, FC),
                              reads=[("wdb", ws), "ah"], writes=[("ps", bank)])
                        S.add("dve", lambda e, xs=xs, bank=bank, n=n: e.tensor_tensor(
                            out=xr[xs][:, :n], in0=psum[:, bank, :n], in1=xr[xs][:, :n], op=ALU.add),
                            reads=[("ps", bank), ("xr", xs)], writes=[("xr", xs)])
                        S.add("sp", lambda e, xs=xs, dch=dch, t0=t0, n=n: e.dma_start(
                            out=xres[dch * 128:(dch + 1) * 128, t0:t0 + n], in_=xr[xs][:, :n]),
                            reads=[("xr", xs)], writes=[("xres", dch, ti)], key=f"xr{xs}")
        S.barrier()

    class AttnBufs:
        pass

    def attn_alloc():
        B = AttnBufs()
        B.e = [A.alloc((512,), F32) for _ in range(3)]
        B.sp = [A.alloc((512,), BF16) for _ in range(4)]
        B.zs = [A.alloc((512,), F32) for _ in range(3)]
        B.t = [A.alloc((512,), F32) for _ in range(2)]
        B.w = [A.alloc((512,), BF16) for _ in range(3)]
        B.R = [A.alloc((512,), BF16) for _ in range(3)]
        B.spn = A.alloc((128,), BF16)
        B.cnt = 0
        return B

    def attn_chain(B, blocks, obank, o_store):
        nb = len(blocks)
        st = {}
        Rprev = None

        def stageA(i):
            nonlocal Rprev
            blk = blocks[i]
            kp, n = blk["kp"], blk["n"]
            g = B.cnt
            B.cnt += 1
            zb = g % 2
            e_s, sp_s, zs_s = g % 3, g % 4, g % 3
            st[i] = dict(g=g, zb=zb, sp_s=sp_s, zs_s=zs_s, R=Rprev)
            S.add("pe", lambda e: blk["zfn"](e, zb), reads=blk["z_reads"], writes=[("ps", zb)])
            S.add("act", lambda e: e.activation(out=B.e[e_s][:kp, :n], in_=psum[:kp, zb, :n], func=AF.Exp, scale=SB_SCALE),
                  reads=[("ps", zb)], writes=[("e", e_s)])
            S.add("dve", lambda e: e.tensor_scalar(out=B.zs[zs_s][:kp, :n], in0=psum[:kp, zb, :n], scalar1=SB_SCALE,
                                                   scalar2=0.0, op0=ALU.mult, op1=ALU.add),
                  reads=[("ps", zb)], writes=[("zs", zs_s)])
            S.add("act", lambda e: e.activation(out=B.sp[sp_s][:kp, :n], in_=B.e[e_s][:kp, :n], func=AF.Ln, bias=1.0),
                  reads=[("e", e_s)], writes=[("sp", sp_s)])
            if blk["mask"] is not None:
                S.add("pool", lambda e: e.tensor_tensor(out=B.sp[sp_s][:kp, :n], in0=B.sp[sp_s][:kp, :n],
                                                        in1=blk["mask"], op=ALU.mult),
                      reads=[("sp", sp_s), "cm"], writes=[("sp", sp_s)])
            if blk.get("save_sp"):
                S.add("pool", lambda e: e.tensor_copy(out=B.spn[:kp, :n], in_=B.sp[sp_s][:kp, :n]),
                      reads=[("sp", sp_s)], writes=["spn"])
            if i + 1 < nb and kp == 128:
                if Rprev is None:
                    Rprev = (B.sp[sp_s], ("sp", sp_s))
                else:
                    r_s = g % 3
                    rp = Rprev
                    S.add("pool", lambda e: e.tensor_tensor(out=B.R[r_s][:, :n], in0=rp[0][:, :n],
                                                            in1=B.sp[sp_s][:, :n], op=ALU.add),
                          reads=[rp[1], ("sp", sp_s)], writes=[("R", r_s)])
                    Rprev = (B.R[r_s], ("R", r_s))

        def stageB(i):
            blk = blocks[i]
            kp, n = blk["kp"], blk["n"]
            s = st[i]
            g = s["g"]
            cb = 2 + g % 2
            t_s, w_s = g % 2, g % 3
            s["w_s"] = w_s
            R = s["R"]
            extra = blk.get("extra_carry", [])

            def cfn(e):
                terms = [(tri[:kp, :kp], B.sp[s["sp_s"]][:kp, :n])]
                if R is not None:
                    terms.append((ones[:, :kp], R[0][:, :n]))
                for (l, r_, _) in extra:
                    terms.append((l, r_))
                ins = None
                for k, (l, r_) in enumerate(terms):
                    ins = e.matmul(psum[:kp, cb, :n], l, r_, start=(k == 0), stop=(k == len(terms) - 1))
                return ins
            rd = [("sp", s["sp_s"]), "cm"] + ([R[1]] if R is not None else []) + [x[2] for x in extra]
            S.add("pe", cfn, reads=rd, writes=[("ps", cb)])
            S.add("dve", lambda e: e.tensor_tensor(out=B.t[t_s][:kp, :n], in0=B.zs[s["zs_s"]][:kp, :n],
                                                   in1=psum[:kp, cb, :n], op=ALU.subtract),
                  reads=[("zs", s["zs_s"]), ("ps", cb)], writes=[("t", t_s)])
            S.add("act", lambda e: e.activation(out=B.w[w_s][:kp, :n], in_=B.t[t_s][:kp, :n], func=AF.Exp),
                  reads=[("t", t_s)], writes=[("w", w_s)])
            if blk["mask"] is not None:
                S.add("pool", lambda e: e.tensor_tensor(out=B.w[w_s][:kp, :n], in0=B.w[w_s][:kp, :n],
                                                        in1=blk["mask"], op=ALU.mult),
                      reads=[("w", w_s), "cm"], writes=[("w", w_s)])

        def stageC(i):
            blk = blocks[i]
            kp, n = blk["kp"], blk["n"]
            s = st[i]
            w_ap = B.w[s["w_s"]]
            S.add("pe", lambda e: blk["pvfn"](e, w_ap, i == 0, i == nb - 1),
                  reads=[("w", s["w_s"])] + blk["pv_reads"], writes=[("ps", obank)])
            if i == nb - 1:
                o_store()

        for it in range(nb + 2):
            if it < nb:
                stageA(it)
            if 1 <= it <= nb:
                stageB(it - 1)
            if it >= 2:
                stageC(it - 2)

    def phase_attn(layer, L):
        A.reset()
        hT = A.alloc((16, NTOK), BF16)
        phase_norm(hT, xres if layer > 0 else xT_in, V_MIX(layer))
        for c in range(KC):
            S.add("sp", lambda e, c=c: e.dma_start(out=hsrc[c], in_=hT[:, c, :]),
                  reads=[("hT", ti, c) for ti in range(5)], writes=[("hsrc", c)], key="hsrc", grp=("hsrc", layer))
        if layer == 0:
            S.add("sp", lambda e: e.dma_start(out=xres, in_=xT_in), writes=["xres_all"], key="xcopy")
        for c in range(KC):
            S.add("pool", lambda e, c=c: e.collective_compute("AllGather", ALU.bypass, replica_groups=GROUPS,
                                                              ins=[hsrc[c].opt()], outs=[hall[c].opt()]),
                  reads=[("hsrc", cc) for cc in range(KC)], writes=[("hall", c)], key=f"ag{layer}", inc=1)
        S.barrier()
        A.reset()
        wq = A.alloc((16, 1536), BF16)
        for c in range(KC):
            S.add("pool", lambda e, c=c: e.dma_start(out=wq[:, c, :], in_=wqkv_in[L, :, c * 1536:(c + 1) * 1536]),
                  writes=[("wq", c)], key="wq", grp=("wq", layer))
        ht = [A.alloc((16, 512), BF16) for _ in range(2)]
        qst = [A.alloc((4, 512), BF16) for _ in range(2)]
        k32 = [A.alloc((4, 512), F32) for _ in range(2)]
        k16 = [A.alloc((4, 512), BF16) for _ in range(2)]
        v32 = [A.alloc((4, 512), F32) for _ in range(2)]
        v16 = [A.alloc((4, 512), BF16) for _ in range(2)]
        wq_reads = [("wq", c) for c in range(KC)]
        tcount = 0
        bcount = 0
        for j in range(4):
            for ti, (t0, n) in enumerate(TILES):
                if tcount >= DBG_NT:
                    continue
                sl = tcount % 2
                tcount += 1
                gcol = j * NTOK + t0
                for c in range(KC):
                    S.add(os.environ.get("MK_HTENG", "pool"), lambda e, sl=sl, j=j, t0=t0, n=n, c=c: e.dma_start(
                        out=ht[sl][:, c, :n], in_=hall[c][j * 128:(j + 1) * 128, t0:t0 + n]),
                        writes=[("ht", sl, c)], key=f"ht{sl}", grp=("ht", layer, tcount))
                rhs = (lambda c, sl=sl, n=n: ht[sl][:, c, :n])
                for nbk in range(8):
                    if "q" not in DBG_SUB:
                        continue
                    bank = bcount % 4
                    bcount += 1
                    S.add("pe", lambda e, bank=bank, n=n, nbk=nbk, rhs=rhs: mm_group(e, bank, n, wq, nbk * 128, rhs, 16),
                          reads=wq_reads + [("ht", sl, c) for c in range(KC)], writes=[("ps", bank)])
                    if nbk < 4:
                        if "a" in DBG_SUB:
                          S.add("act", lambda e, bank=bank, n=n, nbk=nbk, sl=sl: e.activation(
                            out=qst[sl][:, nbk, :n], in_=psum[:, bank, :n], func=AF.Copy),
                            reads=[("ps", bank)], writes=[("qst", sl, nbk)])
                    else:
                        if "a" in DBG_SUB:
                          S.add("act", lambda e, bank=bank, n=n, nbk=nbk, sl=sl: e.activation(
                            out=k32[sl][:, nbk - 4, :n], in_=psum[:, bank, :n], func=AF.Copy),
                            reads=[("ps", bank)], writes=[("k32", sl, nbk)])
                        if "d" in DBG_SUB:
                          S.add("dve", lambda e, bank=bank, n=n, nbk=nbk, sl=sl: e.tensor_copy(
                            out=k16[sl][:, nbk - 4, :n], in_=psum[:, bank, :n]),
                            reads=[("ps", bank)] + ([("k32", sl, nbk)] if "z" in DBG_SUB else []), writes=[("k16", sl, nbk)])
                if "q" in DBG_SUB and "s" in DBG_SUB:
                  S.add("sp", lambda e, sl=sl, gcol=gcol, n=n: e.dma_start(
                    out=qT_d.rearrange("(h p) t -> p h t", p=128)[:, :, gcol:gcol + n], in_=qst[sl][:, :, :n]),
                    reads=[("qst", sl, k) for k in range(4)], writes=[("qT_d", j, ti)], key=f"qst{sl}")
                if "q" in DBG_SUB and "s" in DBG_SUB:
                  S.add("sp", lambda e, sl=sl, gcol=gcol, n=n: e.dma_start(
                    out=k_out[L].rearrange("(h p) t -> p h t", p=128)[:, :, gcol:gcol + n], in_=k32[sl][:, :, :n]),
                    reads=[("k32", sl, k) for k in range(4, 8)], key=f"k32{sl}")
                if "q" in DBG_SUB and "s" in DBG_SUB:
                  S.add("sp", lambda e, sl=sl, gcol=gcol, n=n: e.dma_start(
                    out=kT_d.rearrange("(h p) t -> p h t", p=128)[:, :, gcol:gcol + n], in_=k16[sl][:, :, :n]),
                    reads=[("k16", sl, k) for k in range(4, 8)], writes=[("kT_d", j, ti)], key=f"k16{sl}")
                nsub = max(1, n // 128)
                m = min(n, 128)
                if "v" not in DBG_SUB:
                    continue
                for sub in range(nsub):
                    bank = bcount % 4
                    bcount += 1

                    def vfn(e, bank=bank, sl=sl, sub=sub, m=m):
                        ins = None
                        for c in range(KC):
                            ins = e.matmul(psum[:m, bank, :], ht[sl][:, c, sub * 128:sub * 128 + m],
                                           wq[:, c, 1024:1536], start=(c == 0), stop=(c == KC - 1))
                        return ins
                    S.add("pe", vfn, reads=wq_reads + [("ht", sl, c) for c in range(KC)], writes=[("ps", bank)])
                    S.add("act", lambda e, bank=bank, sl=sl, sub=sub, m=m: e.activation(
                        out=v32[sl][:m, sub, :], in_=psum[:m, bank, :], func=AF.Copy),
                        reads=[("ps", bank)], writes=[("v32", sl, sub)])
                    S.add("dve", lambda e, bank=bank, sl=sl, sub=sub, m=m: e.tensor_copy(
                        out=v16[sl][:m, sub, :], in_=psum[:m, bank, :]),
                        reads=[("ps", bank)], writes=[("v16", sl, sub)])
                if "s" not in DBG_SUB:
                    continue
                S.add("sp", lambda e, sl=sl, gcol=gcol, n=n, nsub=nsub, m=m: e.dma_start(
                    out=v_out[L, gcol:gcol + n, :].rearrange("(s p) d -> p s d", p=m), in_=v32[sl][:m, :nsub, :]),
                    reads=[("v32", sl, k) for k in range(nsub)], key=f"v32{sl}")
                S.add("sp", lambda e, sl=sl, gcol=gcol, n=n, nsub=nsub, m=m: e.dma_start(
                    out=v_d[gcol:gcol + n, :].rearrange("(s p) d -> p s d", p=m), in_=v16[sl][:m, :nsub, :]),
                    reads=[("v16", sl, k) for k in range(nsub)], writes=[("v_d", j, ti)], key=f"v16{sl}")
        S.barrier()
        A.reset()
        B = attn_alloc()
        qh = A.alloc((SEQ,), BF16)
        kh = A.alloc((SEQ,), BF16)
        vh = A.alloc((64, 128), BF16)
        ost = [A.alloc((512,), BF16) for _ in range(2)]
        kc_ = A.alloc((8, 1024), BF16)
        vc_ = A.alloc((8, 8, 128), BF16)
        qn = A.alloc((128,), BF16)
        kn = A.alloc((128,), BF16)
        vn = A.alloc((8, 128), BF16)
        ocount = 0
        for hl in range(4):
            hr = slice(hl * 128, (hl + 1) * 128)
            for j in range(4):
                S.add("sp", lambda e, j=j, hr=hr: e.dma_start(out=qh[:, j * PT:(j + 1) * PT],
                                                              in_=qT_d[hr, j * NTOK:j * NTOK + PT]),
                      reads=[("qT_d", j, ti) for ti in range(4)], writes=[("qh", j)], key="qh", grp=("qh", layer, hl))
                S.add("sp", lambda e, j=j, hr=hr: e.dma_start(out=kh[:, j * PT:(j + 1) * PT],
                                                              in_=kT_d[hr, j * NTOK:j * NTOK + PT]),
                      reads=[("kT_d", j, ti) for ti in range(4)], writes=[("kh", j)], key="kh", grp=("kh", layer, hl))
                S.add("sp", lambda e, j=j, hr=hr: e.dma_start(
                    out=vh[:, j * 16:(j + 1) * 16, :],
                    in_=v_d[j * NTOK:j * NTOK + PT, hr].rearrange("(i p) d -> p i d", p=128)),
                    reads=[("v_d", j, ti) for ti in range(4)], writes=[("vh", j)], key="vh", grp=("vh", layer, hl))
                S.add("sp", lambda e, j=j, hr=hr: e.dma_start(out=qn[:, j * 32:(j + 1) * 32],
                                                              in_=qT_d[hr, j * NTOK + PT:(j + 1) * NTOK]),
                      reads=[("qT_d", j, 4)], writes=[("qn", j)], key="qn", grp=("qn", layer, hl))
                S.add("sp", lambda e, j=j, hr=hr: e.dma_start(out=kn[:, j * 32:(j + 1) * 32],
                                                              in_=kT_d[hr, j * NTOK + PT:(j + 1) * NTOK]),
                      reads=[("kT_d", j, 4)], writes=[("kn", j)], key="kn", grp=("kn", layer, hl))
                S.add("sp", lambda e, j=j, hr=hr: e.dma_start(
                    out=vn[:16, 2 * j:2 * j + 2, :],
                    in_=v_d[j * NTOK + PT:(j + 1) * NTOK, hr].rearrange("(s i) d -> i s d", i=16)),
                    reads=[("v_d", j, 4)], writes=[("vn", j)], key="vn", grp=("vn", layer, hl))
            S.add("pool", lambda e, hl=hl: e.dma_start(out=kc_, in_=ckT_in[L, hl].rearrange("p (s k) -> p s k", s=8),
                                                       max_dma_last_dim=8192),
                  writes=["kc"], key="kc")
            S.add("pool", lambda e, hl=hl: e.dma_start(
                out=vc_, in_=cv_in[L, hl].rearrange("p (s b d) -> p s b d", s=8, b=8), max_dma_last_dim=8192),
                writes=["vc"], key="vc")
            q_reads = [("qh", j) for j in range(4)]
            k_reads = [("kh", j) for j in range(4)]
            v_reads = [("vh", j) for j in range(4)]
            for g in range(16):
                obank = 4 + g % 2
                blocks = []
                for jb in range(4 * g + 3, -1, -1):
                    m = jb - 4 * g
                    blk = dict(kp=128, n=512)
                    blk["zfn"] = (lambda e, zb, jb=jb, g=g: e.matmul(
                        psum[:, zb, :], kh[:, jb * 128:(jb + 1) * 128], qh[:, g * 512:(g + 1) * 512],
                        start=True, stop=True))
                    blk["z_reads"] = q_reads + k_reads
                    blk["mask"] = masks[:, m, :] if m >= 0 else None
                    blk["pvfn"] = (lambda e, w_ap, first, last, jb=jb, obank=obank: e.matmul(
                        psum[:, obank, :], vh[:, jb, :], w_ap[:, :512], start=first, stop=last))
                    blk["pv_reads"] = v_reads
                    blocks.append(blk)
                osl = ocount % 2
                ocount += 1
                jq, tq = g // 4, (g % 4) * 512

                def o_store(obank=obank, osl=osl, jq=jq, tq=tq, hr=hr):
                    S.add("act", lambda e: e.activation(out=ost[osl], in_=psum[:, obank, :], func=AF.Copy),
                          reads=[("ps", obank)], writes=[("ost", osl)])
                    S.add("sp", lambda e: e.dma_start(out=oT_d[hr, jq * NTOK + tq:jq * NTOK + tq + 512], in_=ost[osl]),
                          reads=[("ost", osl)], writes=[("oT_d", hl, jq, tq)], key=f"ost{osl}")
                attn_chain(B, blocks, obank, o_store)
            obank = 6
            blocks = []
            qn_reads = [("qn", j) for j in range(4)]
            kn_reads = [("kn", j) for j in range(4)]
            vn_reads = [("vn", j) for j in range(4)]

            def z_new(e, zb):
                ins = None
                for s_ in range(8):
                    cs = slice(s_ * 16, s_ * 16 + 16)
                    ins = e.matmul(psum[:16, zb, cs], kn[:, cs], qn[:, cs], start=True, stop=True)
                return ins

            def pv_new(e, w_ap, first, last):
                ins = e.matmul(psum[:, obank, :128], ones[:16, :], zt[:16, :128], start=True, stop=False)
                for s_ in range(8):
                    cs = slice(s_ * 16, s_ * 16 + 16)
                    ins = e.matmul(psum[:, obank, cs], vn[:16, s_, :], w_ap[:16, cs], start=False, stop=False)
                return ins
            blocks.append(dict(kp=16, n=128, zfn=z_new, z_reads=qn_reads + kn_reads, mask=mask16[:16, :],
                               pvfn=pv_new, pv_reads=vn_reads + ["zt", "cm"]))
            for cbk in range(7, -1, -1):
                def z_c(e, zb, cbk=cbk):
                    ins = None
                    for s_ in range(8):
                        cs = slice(s_ * 16, s_ * 16 + 16)
                        ins = e.matmul(psum[:, zb, cs], kc_[:, s_, cbk * 128:(cbk + 1) * 128], qn[:, cs],
                                       start=True, stop=True)
                    return ins

                def pv_c(e, w_ap, first, last, cbk=cbk):
                    ins = None
                    for s_ in range(8):
                        cs = slice(s_ * 16, s_ * 16 + 16)
                        ins = e.matmul(psum[:, obank, cs], vc_[:, s_, cbk, :], w_ap[:, cs], start=False, stop=last)
                    return ins
                blocks.append(dict(kp=128, n=128, zfn=z_c, z_reads=qn_reads + ["kc"], mask=None,
                                   pvfn=pv_c, pv_reads=["vc"]))
            blocks[0]["save_sp"] = True
            for bi in range(1, 9):
                blocks[bi]["extra_carry"] = [(ones[:16, :], B.spn[:16, :128], "spn")]
            osl = ocount % 2
            ocount += 1

            def o_store_s(osl=osl, hr=hr):
                S.add("act", lambda e: e.activation(out=ost[osl][:, :128], in_=psum[:, obank, :128], func=AF.Copy),
                      reads=[("ps", obank)], writes=[("ost", osl)])
                for j in range(4):
                    S.add("sp", lambda e, j=j: e.dma_start(out=oT_d[hr, j * NTOK + PT:(j + 1) * NTOK],
                                                           in_=ost[osl][:, j * 32:(j + 1) * 32]),
                          reads=[("ost", osl)], writes=[("oT_d", hl, j, "s")], key=f"ost{osl}", grp=("osts", layer, hl))
            attn_chain(B, blocks, obank, o_store_s)
        S.barrier()
        A.reset()
        wo = A.alloc((4, D), BF16)
        for hl in range(4):
            S.add("pool", lambda e, hl=hl: e.dma_start(out=wo[:, hl, :], in_=wo_in[L, :, hl * D:(hl + 1) * D]),
                  writes=[("wo", hl)], key="wo", grp=("wo", layer))
        ot = [A.alloc((4, 512), BF16) for _ in range(2)]
        pst = [A.alloc((16, 512), F32) for _ in range(2)]
        tcount = 0
        bcount = 0
        for j in range(4):
            for ti, (t0, n) in enumerate(TILES):
                sl = tcount % 2
                tcount += 1
                gcol = j * NTOK + t0
                S.add("sp", lambda e, sl=sl, gcol=gcol, n=n: e.dma_start(
                    out=ot[sl][:, :, :n], in_=oT_d.rearrange("(h p) t -> p h t", p=128)[:, :, gcol:gcol + n]),
                    writes=[("ot", sl)], key=f"ot{sl}")
                for fc in range(KC):
                    bank = bcount % 4
                    bcount += 1

                    def wfn(e, bank=bank, sl=sl, n=n, fc=fc):
                        ins = None
                        for hl in range(4):
                            ins = e.matmul(psum[:, bank, :n], wo[:, hl, fc * 128:(fc + 1) * 128], ot[sl][:, hl, :n],
                                           start=(hl == 0), stop=(hl == 3))
                        return ins
                    S.add("pe", wfn, reads=[("wo", h) for h in range(4)] + [("ot", sl)], writes=[("ps", bank)])
                    eng = "act" if fc % 2 == 0 else "dve"
                    if eng == "act":
                        S.add("act", lambda e, bank=bank, sl=sl, n=n, fc=fc: e.activation(
                            out=pst[sl][:, fc, :n], in_=psum[:, bank, :n], func=AF.Copy),
                            reads=[("ps", bank)], writes=[("pst", sl, fc)])
                    else:
                        S.add("dve", lambda e, bank=bank, sl=sl, n=n, fc=fc: e.tensor_copy(
                            out=pst[sl][:, fc, :n], in_=psum[:, bank, :n]),
                            reads=[("ps", bank)], writes=[("pst", sl, fc)])
                S.add("sp", lambda e, sl=sl, j=j, t0=t0, n=n: e.dma_start(
                    out=fm(wop_d[j * D:(j + 1) * D, :])[:, :, t0:t0 + n], in_=pst[sl][:, :, :n]),
                    reads=[("pst", sl, fc) for fc in range(KC)], writes=[("wop", j, ti)], key=f"pst{sl}")
        S.barrier()
        S.add("pool", lambda e: e.collective_compute("ReduceScatter", ALU.add, replica_groups=GROUPS,
                                                     ins=[wop_d.opt()], outs=[wosum_d.opt()]),
              key=f"rs{layer}", inc=1)
        S.barrier()

    def phase_conv(layer, L):
        A.reset()
        hT = A.alloc((16, NTOK), BF16)
        mark = A.off
        phase_norm(hT, xres, V_MIX(layer))
        S.barrier()
        A.off = mark
        wb = [A.alloc((16, 256), BF16) for _ in range(3)]
        sg = [A.alloc((512,), F32) for _ in range(2)]
        ust = [A.alloc((NTOK,), F32) for _ in range(2)]
        pc = 0
        def pw1_load(fc):
            load_wblock(wb[fc % 3], wpw1_in[L, fc], 16, 256, ("wb", fc % 3), f"wb{fc % 3}")
        pw1_load(0)
        pw1_load(1)
        for fc in range(KC):
            ws = fc % 3
            if fc + 2 < KC:
                pw1_load(fc + 2)
            us = fc % 2
            for ti, (t0, n) in enumerate(TILES):
                ba = 2 * (pc % 2)
                bg = ba + 1
                pc += 1
                s = sg[pc % 2]
                rhs = (lambda c, t0=t0, n=n: hT[:, c, t0:t0 + n])
                S.add("pe", lambda e, ba=ba, n=n, w=wb[ws], rhs=rhs: mm_group(e, ba, n, w, 0, rhs, 16),
                      reads=[("wb", ws)] + [("hT", ti, c) for c in range(KC)], writes=[("ps", ba)])
                S.add("pe", lambda e, bg=bg, n=n, w=wb[ws], rhs=rhs: mm_group(e, bg, n, w, 128, rhs, 16),
                      reads=[("wb", ws)] + [("hT", ti, c) for c in range(KC)], writes=[("ps", bg)])
                S.add("act", lambda e, s=s, bg=bg, n=n, fc=fc: e.activation(
                    out=s[:, :n], in_=psum[:, bg, :n], func=AF.Sigmoid, bias=vecs[:, V_BPW1(L, 1), fc:fc + 1]),
                    reads=[("ps", bg), "vecs"], writes=[("sg", pc % 2)])
                S.add("dve", lambda e, s=s, ba=ba, n=n, fc=fc, us=us, t0=t0: e.scalar_tensor_tensor(
                    out=ust[us][:, t0:t0 + n], in0=psum[:, ba, :n], scalar=vecs[:, V_BPW1(L, 0), fc:fc + 1],
                    in1=s[:, :n], op0=ALU.add, op1=ALU.mult),
                    reads=[("ps", ba), ("sg", pc % 2), "vecs"], writes=[("ust", us, ti)])
            S.add("sp", lambda e, us=us, fc=fc: e.dma_start(out=u_d[fc * 128:(fc + 1) * 128, :], in_=ust[us]),
                  reads=[("ust", us, ti) for ti in range(5)], writes=[("u_d", fc)], key=f"ust{us}", grp=("ust", layer, fc))
            S.add("sp", lambda e, us=us, fc=fc: e.dma_start(out=utail_d[fc * 128:(fc + 1) * 128, :],
                                                            in_=ust[us][:, PT - 30:PT]),
                  reads=[("ust", us, ti) for ti in range(5)], writes=[("utail", fc)], key=f"ust{us}", grp=("ust", layer, fc))
            S.add("sp", lambda e, us=us, fc=fc: e.dma_start(out=convp_out[L, fc * 128:(fc + 1) * 128, :],
                                                            in_=ust[us][:, PT - 30:PT]),
                  reads=[("ust", us, ti) for ti in range(5)], key=f"ust{us}", grp=("ust", layer, fc))
        S.barrier()
        S.add("pool", lambda e: e.collective_compute("AllGather", ALU.bypass, replica_groups=GROUPS,
                                                     ins=[utail_d.opt()], outs=[uhalo_d.opt()]),
              key=f"agc{layer}", inc=1)
        S.barrier()
        A.reset()
        U = [A.alloc((30 + PT,), F32) for _ in range(2)]
        US = [A.alloc((2, 46), F32) for _ in range(2)]
        hal = [A.alloc((4, 30), F32) for _ in range(2)]
        yb = [A.alloc((NTOK,), F32) for _ in range(2)]
        for fc in range(KC):
            sl = fc % 2
            eng = "dve"
            fr = slice(fc * 128, (fc + 1) * 128)
            S.add("sp", lambda e, sl=sl, fr=fr: e.dma_start(out=U[sl][:, 30:], in_=u_d[fr, 0:PT]),
                  writes=[("U", sl)], key=f"U{sl}")
            S.add("sp", lambda e, sl=sl, fr=fr: e.dma_start(
                out=US[sl][:, :, 30:46], in_=u_d[fr, PT:NTOK].rearrange("p (s i) -> p s i", s=2)),
                writes=[("USu", sl)], key=f"USu{sl}")
            S.add("sp", lambda e, sl=sl, fr=fr: e.dma_start(
                out=US[sl][:, :, 0:30], in_=stT_in[L, fr, :].rearrange("p (s w) -> p s w", s=2)),
                writes=[("USb", sl)], key=f"USb{sl}")
            S.add("sp", lambda e, sl=sl, fc=fc: e.dma_start(
                out=hal[sl], in_=uhalo_d.rearrange("(j c p) w -> p c j w", j=4, p=128)[:, fc, :, :]),
                writes=[("hal", sl)], key=f"hal{sl}")
            S.add(eng, lambda e, sl=sl: e.tensor_scalar(out=U[sl][:, 0:30], in0=hal[sl][:, 0, :], scalar1=sel[:, 0:1],
                                                        scalar2=0.0, op0=ALU.mult, op1=ALU.add),
                  reads=[("hal", sl), "sel"], writes=[("Uh", sl)])
            for jj in range(1, 4):
                S.add(eng, lambda e, sl=sl, jj=jj: e.scalar_tensor_tensor(
                    out=U[sl][:, 0:30], in0=hal[sl][:, jj, :], scalar=sel[:, jj:jj + 1], in1=U[sl][:, 0:30],
                    op0=ALU.mult, op1=ALU.add),
                    reads=[("hal", sl), "sel", ("Uh", sl)], writes=[("Uh", sl)])
            S.add("sp", lambda e, sl=sl, fr=fr: e.dma_start(
                out=convs_out[L, fr, :].rearrange("p (s w) -> p s w", s=2), in_=US[sl][:, :, 16:46]),
                reads=[("USu", sl), ("USb", sl)], key=f"cso{sl}")
            for w in range(CW):
                wv = vecs[:, V_WDW(L, w), fc:fc + 1]
                if w == 0:
                    S.add(eng, lambda e, sl=sl, wv=wv, fc=fc: e.tensor_scalar(
                        out=yb[sl][:, 0:PT], in0=U[sl][:, 0:PT], scalar1=wv, scalar2=vecs[:, V_BDW(L), fc:fc + 1],
                        op0=ALU.mult, op1=ALU.add),
                        reads=[("U", sl), ("Uh", sl), "vecs"], writes=[("yb", sl)])
                    S.add(eng, lambda e, sl=sl, wv=wv, fc=fc: e.tensor_scalar(
                        out=yb[sl][:, PT:NTOK].rearrange("p (s i) -> p s i", s=2), in0=US[sl][:, :, 0:16], scalar1=wv,
                        scalar2=vecs[:, V_BDW(L), fc:fc + 1], op0=ALU.mult, op1=ALU.add),
                        reads=[("USu", sl), ("USb", sl), "vecs"], writes=[("ybs", sl)])
                else:
                    S.add(eng, lambda e, sl=sl, wv=wv, w=w: e.scalar_tensor_tensor(
                        out=yb[sl][:, 0:PT], in0=U[sl][:, w:w + PT], scalar=wv, in1=yb[sl][:, 0:PT],
                        op0=ALU.mult, op1=ALU.add),
                        reads=[("U", sl), ("Uh", sl), "vecs", ("yb", sl)], writes=[("yb", sl)])
                    S.add(eng, lambda e, sl=sl, wv=wv, w=w: e.scalar_tensor_tensor(
                        out=yb[sl][:, PT:NTOK].rearrange("p (s i) -> p s i", s=2), in0=US[sl][:, :, w:w + 16],
                        scalar=wv, in1=yb[sl][:, PT:NTOK].rearrange("p (s i) -> p s i", s=2),
                        op0=ALU.mult, op1=ALU.add),
                        reads=[("USu", sl), ("USb", sl), "vecs", ("ybs", sl)], writes=[("ybs", sl)])
            S.add("sp", lambda e, sl=sl, fr=fr: e.dma_start(out=yc_d[fr, :], in_=yb[sl]),
                  reads=[("yb", sl), ("ybs", sl)], writes=[("yc_d", fc)], key=f"yb{sl}")
        S.barrier()
        A.reset()
        sT = A.alloc((16, NTOK), BF16)
        mark = A.off
        phase_norm(sT, yc_d, None, mode="ln", ln=(V_LNG(L), V_LNB(L)))
        S.barrier()
        A.off = mark
        wb2 = [A.alloc((16, 256), BF16) for _ in range(3)]
        xr = [A.alloc((512,), F32) for _ in range(3)]
        xcount = 0
        def pw2_load(db):
            load_wblock(wb2[db % 3], wpw2_in[L, db], 16, 256, ("wb", db % 3), f"wb{db % 3}")
        pw2_load(0)
        pw2_load(1)
        for db in range(8):
            ws = db % 3
            if db + 2 < 8:
                pw2_load(db + 2)
            for ti, (t0, n) in enumerate(TILES):
                for q in range(2):
                    dch = db * 2 + q
                    bank = xcount % 4
                    xs = xcount % 3
                    xcount += 1
                    S.add("sp", lambda e, xs=xs, dch=dch, t0=t0, n=n: e.dma_start(
                        out=xr[xs][:, :n], in_=xres[dch * 128:(dch + 1) * 128, t0:t0 + n]),
                        writes=[("xr", xs)], key=f"xr{xs}")
                    rhs = (lambda c, t0=t0, n=n: sT[:, c, t0:t0 + n])
                    S.add("pe", lambda e, bank=bank, n=n, w=wb2[ws], q=q, rhs=rhs: mm_group(e, bank, n, w, q * 128, rhs, 16),
                          reads=[("wb", ws)] + [("hT", ti, c) for c in range(KC)], writes=[("ps", bank)])
                    S.add("dve", lambda e, xs=xs, bank=bank, n=n, dch=dch: e.scalar_tensor_tensor(
                        out=xr[xs][:, :n], in0=psum[:, bank, :n], scalar=vecs[:, V_BPW2(L), dch:dch + 1],
                        in1=xr[xs][:, :n], op0=ALU.add, op1=ALU.add),
                        reads=[("ps", bank), ("xr", xs), "vecs"], writes=[("xr", xs)])
                    S.add("sp", lambda e, xs=xs, dch=dch, t0=t0, n=n: e.dma_start(
                        out=xres[dch * 128:(dch + 1) * 128, t0:t0 + n], in_=xr[xs][:, :n]),
                        reads=[("xr", xs)], key=f"xr{xs}")
        S.barrier()

    for layer in range(4):
        if S.stopped:
            break
        if layer % 2 == 0:
            phase_attn(layer, layer // 2)
            if S.stopped:
                break
            phase_ffn(layer, add=wosum_d)
        else:
            phase_conv(layer, layer // 2)
            if S.stopped:
                break
            phase_ffn(layer, add=None)
    if not S.stopped:
        A.reset()
        phase_norm(None, xres, V_FIN, final_out=y_out)
    S.barrier(engines=("sp",), force=True)

    nsem = S.assign()
    with ExitStack() as st:
        sems = [st.enter_context(nc.semaphore(f"s{i}")) for i in range(nsem)]
        block = st.enter_context(nc.Block())

        @block.tensor
        def _(e):
            S.run_engine("pe", e, sems)

        @block.scalar
        def _(e):
            S.run_engine("act", e, sems)

        @block.vector
        def _(e):
            S.run_engine("dve", e, sems)

        @block.gpsimd
        def _(e):
            S.run_engine("pool", e, sems)

        @block.sync
        def _(e):
            S.run_engine("sp", e, sems)
    return nc


def _wblocks(w, nw):
    K, N = w.shape
    return np.ascontiguousarray(w.reshape(K // 128, 128, N // nw, nw).transpose(2, 1, 0, 3)).reshape(N // nw, 128, -1)


def _consts():
    k = np.arange(128)
    ones = np.ones((128, 128), np.float32)
    tri = (k[:, None] >= k[None, :]).astype(np.float32)
    q = np.arange(512)
    masks = np.stack([((128 * m + k[:, None]) < q[None, :]).astype(np.float32) for m in range(4)], 1)
    i = np.arange(128) % 16
    m16 = (k[:, None] < i[None, :]).astype(np.float32)
    return np.concatenate([ones, tri, masks.reshape(128, 2048), m16], 1)


_NC_CACHE = {}


def kernel(**inputs):
    in_maps = _host_inputs(**inputs)
    if "nc" not in _NC_CACHE:
        _NC_CACHE["nc"] = build_program()
    res = run_bass_kernel_spmd(_NC_CACHE["nc"], in_maps, core_ids=list(range(8)))
    return _assemble(res.results)


def _host_inputs(x_prompt, x_sample, cache_k, cache_v, state_conv, norm_mix_g, norm_ffn_g,
                 w_qkv, w_o, w_pw1, b_pw1, w_dw, b_dw, ln_g, ln_b, w_pw2, b_pw2,
                 w_gate_up, w_down, final_norm_g):
    f = lambda a: np.asarray(a, dtype=np.float32)
    x_prompt, x_sample, cache_k, cache_v, state_conv = map(f, (x_prompt, x_sample, cache_k, cache_v, state_conv))
    w_qkv, w_o, w_pw1, w_pw2, w_gate_up, w_down = map(f, (w_qkv, w_o, w_pw1, w_pw2, w_gate_up, w_down))
    vl = [f(norm_mix_g[i]) for i in range(4)] + [f(norm_ffn_g[i]) for i in range(4)] + [f(final_norm_g)]
    b1 = f(b_pw1)
    for j in range(2):
        vl += [b1[j, :D], b1[j, D:]]
    vl += [f(b_dw)[j] for j in range(2)] + [f(ln_g)[j] for j in range(2)] + [f(ln_b)[j] for j in range(2)]
    vl += [f(b_pw2)[j] for j in range(2)]
    for j in range(2):
        vl += [f(w_dw)[j, w] for w in range(CW)]
    assert len(vl) == NV
    vecs = np.ascontiguousarray(np.stack(vl, 0).reshape(NV, 16, 128).transpose(2, 0, 1)).reshape(128, NV * 16)
    cmat = _consts()
    return _prepare(locals())


def _prepare(a):
    x_prompt, x_sample, cache_k, cache_v, state_conv = (a[k] for k in ("x_prompt", "x_sample", "cache_k", "cache_v", "state_conv"))
    w_qkv, w_o, w_pw1, w_pw2, w_gate_up, w_down = (a[k] for k in ("w_qkv", "w_o", "w_pw1", "w_pw2", "w_gate_up", "w_down"))
    vecs, cmat = a["vecs"], a["cmat"]
    if LITE:
        return _prepare_cores(x_prompt, x_sample, cache_k, cache_v, state_conv, w_qkv, w_o, vecs, cmat, {})
    wgu = np.empty((4, 22, 128, 16 * 512), np.float32)
    for l in range(4):
        g_, u_ = w_gate_up[l][:, :DFF], w_gate_up[l][:, DFF:]
        gb = g_.reshape(16, 128, 22, 2, 128)
        ub = u_.reshape(16, 128, 22, 2, 128)
        cat = np.concatenate([gb, ub], axis=3)
        wgu[l] = cat.transpose(2, 1, 0, 3, 4).reshape(22, 128, 16 * 512)
    wd = np.stack([_wblocks(w_down[l], 256) for l in range(4)], 0)
    wpw1 = np.empty((2, 16, 128, 16 * 256), np.float32)
    for l in range(2):
        a_, g_ = w_pw1[l][:, :D], w_pw1[l][:, D:]
        ab = a_.reshape(16, 128, 16, 1, 128)
        gb = g_.reshape(16, 128, 16, 1, 128)
        cat = np.concatenate([ab, gb], axis=3)
        wpw1[l] = cat.transpose(2, 1, 0, 3, 4).reshape(16, 128, 16 * 256)
    wpw2 = np.stack([_wblocks(w_pw2[l], 256) for l in range(2)], 0)
    return _prepare_cores(x_prompt, x_sample, cache_k, cache_v, state_conv, w_qkv, w_o, vecs, cmat,
                          dict(wgu=wgu, wd=wd, wpw1=wpw1, wpw2=wpw2))


def _prepare_cores(x_prompt, x_sample, cache_k, cache_v, state_conv, w_qkv, w_o, vecs, cmat, shared):
    in_maps = []
    for c in range(8):
        b, r = c // 4, c % 4
        xT = np.empty((D, NTOK), np.float32)
        xT[:, :PT] = x_prompt[b, r * PT:(r + 1) * PT, :].T
        xT[:, PT:] = x_sample[8 * b + 2 * r:8 * b + 2 * r + 2].reshape(ST, D).T
        sel = np.zeros((128, 4), np.float32)
        if r > 0:
            sel[:, r - 1] = 1.0
        hs = slice(4 * r * 128, (4 * r + 4) * 128)
        wq = np.empty((2, 128, 16 * 1536), np.float32)
        wo = np.empty((2, 128, 4 * D), np.float32)
        ckT = np.empty((2, 4, 128, 8 * 1024), np.float32)
        cv = np.empty((2, 4, 128, 8 * 8 * 128), np.float32)
        for l in range(2):
            cols = np.concatenate([w_qkv[l][:, wh * D:(wh + 1) * D][:, hs] for wh in range(3)], 1)
            wq[l] = cols.reshape(16, 128, 1536).transpose(1, 0, 2).reshape(128, -1)
            wo[l] = w_o[l][hs, :].reshape(4, 128, D).transpose(1, 0, 2).reshape(128, -1)
            ck = cache_k[l, 8 * b:8 * b + 8, :, 4 * r:4 * r + 4, :]
            ckT[l] = ck.transpose(2, 3, 0, 1).reshape(4, 128, -1)
            cvv = cache_v[l, 8 * b:8 * b + 8, :, 4 * r:4 * r + 4, :].reshape(8, 8, 128, 4, 128)
            cv[l] = cvv.transpose(3, 2, 0, 1, 4).reshape(4, 128, -1)
        stT = np.ascontiguousarray(
            state_conv[:, 8 * b + 2 * r:8 * b + 2 * r + 2].transpose(0, 3, 1, 2)).reshape(2, D, 60)
        m = dict(xT=xT, vecs=vecs, cmat=cmat, sel=sel, wqkv=wq, wo=wo, ckT=ckT, cv=cv, stT=stT)
        m.update(shared)
        in_maps.append(m)
    return in_maps


def _assemble(R):
    y_prompt = np.empty((2, SEQ, D), np.float32)
    y_sample = np.empty((16, 16, D), np.float32)
    k_p = np.empty((2, 2, SEQ, NH, 128), np.float32)
    v_p = np.empty((2, 2, SEQ, NH, 128), np.float32)
    conv_p = np.empty((2, 2, 30, D), np.float32)
    k_s = np.empty((2, 16, 16, NH, 128), np.float32)
    v_s = np.empty((2, 16, 16, NH, 128), np.float32)
    conv_s = np.empty((2, 16, 30, D), np.float32)
    for c in range(8):
        b, r = c // 4, c % 4
        o = R[c]
        y = np.asarray(o["y_out"])
        y_prompt[b, r * PT:(r + 1) * PT] = y[:, :PT].T
        y_sample[8 * b + 2 * r:8 * b + 2 * r + 2] = y[:, PT:].T.reshape(2, 16, D)
        ko = np.asarray(o["k_out"]).reshape(2, 4, 128, 4, NTOK)
        vo = np.asarray(o["v_out"]).reshape(2, 4, NTOK, 4, 128)
        for l in range(2):
            k_p[l, b, :, 4 * r:4 * r + 4, :] = ko[l][:, :, :, :PT].transpose(2, 3, 0, 1).reshape(SEQ, 4, 128)
            v_p[l, b, :, 4 * r:4 * r + 4, :] = vo[l][:, :PT].reshape(SEQ, 4, 128)
            ks = ko[l][:, :, :, PT:].reshape(4, 128, 4, 2, 16)
            k_s[l, 8 * b:8 * b + 8, :, 4 * r:4 * r + 4, :] = ks.transpose(2, 3, 4, 0, 1).reshape(8, 16, 4, 128)
            vs = vo[l][:, PT:].reshape(4, 2, 16, 4, 128)
            v_s[l, 8 * b:8 * b + 8, :, 4 * r:4 * r + 4, :] = vs.reshape(8, 16, 4, 128)
            if r == 3:
                conv_p[l, b] = np.asarray(o["convp_out"])[l].T
            cs = np.asarray(o["convs_out"])[l].reshape(D, 2, 30)
            conv_s[l, 8 * b + 2 * r:8 * b + 2 * r + 2] = cs.transpose(1, 2, 0)
    return (y_prompt, y_sample, k_p, v_p, conv_p, k_s, v_s, conv_s)
```
